# Optimizing a Trainium2 kernel written in Bass

```python
import math
import numpy as np
import jax
import jax.numpy as jnp
from jax import lax

D_MODEL = 1024
BATCH = 16
SEQ = 2048
DEPTH = 2

EXPAND = 2
MIX_WIDTH = EXPAND * D_MODEL
HG_WIDTH = MIX_WIDTH // 2
HG_HEAD_DIM = 128
HG_HEADS = HG_WIDTH // HG_HEAD_DIM
HG_CHUNK = 64
ATT_WIDTH = MIX_WIDTH - HG_WIDTH
ATT_HEAD_DIM = 64
ATT_HEADS = ATT_WIDTH // ATT_HEAD_DIM
ATT_KV_HEADS = max(1, ATT_HEADS // 8)
ATT_GROUP = ATT_HEADS // ATT_KV_HEADS
KV_WIDTH = ATT_KV_HEADS * ATT_HEAD_DIM
WINDOW = 128
ATT_BLOCK = 128
ATT_SCALE = 1.0 / math.sqrt(ATT_HEAD_DIM)
ROPE_THETA = 10000.0
NORM_EPS = 1e-6
NEG_INF = -1e30
LB_FLOOR = 1e-20

SPLIT_SIZES = (HG_WIDTH, HG_WIDTH, HG_WIDTH, HG_WIDTH, ATT_WIDTH, KV_WIDTH, KV_WIDTH, ATT_WIDTH)
IN_WIDTH = int(sum(SPLIT_SIZES))
SPLIT_POINTS = tuple(int(v) for v in np.cumsum(SPLIT_SIZES)[:-1])

kernel_name = "hymba_hgrn2_swa_sink_hybrid"


def rms_norm(x, g):
    xf = x.astype(jnp.float32)
    y = xf * lax.rsqrt(jnp.mean(xf * xf, axis=-1, keepdims=True) + NORM_EPS)
    return (y * g.astype(jnp.float32)).astype(x.dtype)


def rotary(x, pos):
    half = x.shape[-1] // 2
    inv_freq = ROPE_THETA ** (-jnp.arange(half, dtype=jnp.float32) / half)
    ang = pos.astype(jnp.float32)[:, None] * inv_freq[None, :]
    cos = jnp.cos(ang)[None, :, None, :]
    sin = jnp.sin(ang)[None, :, None, :]
    xf = x.astype(jnp.float32)
    x1, x2 = xf[..., :half], xf[..., half:]
    return jnp.concatenate([x1 * cos - x2 * sin, x2 * cos + x1 * sin], axis=-1).astype(x.dtype)


def hgrn2(q, f_logit, i, lb):
    B, S, _ = q.shape
    C, H, D = HG_CHUNK, HG_HEADS, HG_HEAD_DIM
    nC = S // C
    lb = lb.astype(jnp.float32)
    qf = jax.nn.silu(q.astype(jnp.float32))
    logf = jnp.logaddexp(jnp.log(jnp.maximum(lb, LB_FLOOR)),
                         jnp.log1p(-lb) + jax.nn.log_sigmoid(f_logit.astype(jnp.float32)))
    k = -jnp.expm1(logf)

    def chunks(t):
        return t.reshape(B, nC, C, H, D).transpose(1, 0, 3, 2, 4)

    qc, kc, vc, gc = chunks(qf), chunks(k), chunks(i.astype(jnp.float32)), chunks(logf)
    bc = jnp.cumsum(gc, axis=3)
    causal = jnp.tril(jnp.ones((C, C), dtype=bool))[None, None, :, :, None]

    def step(state, xs):
        qch, kch, vch, b = xs
        b_last = b[:, :, C - 1:C, :]
        diff = b[:, :, :, None, :] - b[:, :, None, :, :]
        decay = jnp.where(causal, jnp.exp(jnp.where(causal, diff, 0.0)), 0.0)
        scores = jnp.einsum('bhtd,bhsd,bhtsd->bhts', qch, kch, decay)
        o = jnp.einsum('bhts,bhsv->bhtv', scores, vch)
        o = o + jnp.einsum('bhtd,bhdv->bhtv', qch * jnp.exp(b), state)
        state = jnp.exp(b_last[:, :, 0, :])[..., None] * state + \
            jnp.einsum('bhsd,bhsv->bhdv', kch * jnp.exp(b_last - b), vch)
        return state, o

    s0 = jnp.zeros((B, H, D, D), dtype=jnp.float32)
    _, o = lax.scan(step, s0, (qc, kc, vc, bc))
    return o.transpose(1, 0, 3, 2, 4).reshape(B, S, H, D)


def sliding_window_attention(q, k, v, sinks):
    B, S = q.shape[0], q.shape[1]
    L, KV, G, D = ATT_BLOCK, ATT_KV_HEADS, ATT_GROUP, ATT_HEAD_DIM
    nB = S // L
    qb = q.reshape(B, nB, L, KV, G, D)
    kb = k.reshape(B, nB, L, KV, D)
    vb = v.reshape(B, nB, L, KV, D)
    k_prev = jnp.concatenate([jnp.zeros_like(kb[:, :1]), kb[:, :-1]], axis=1)
    v_prev = jnp.concatenate([jnp.zeros_like(vb[:, :1]), vb[:, :-1]], axis=1)
    kw = jnp.concatenate([k_prev, kb], axis=2)
    vw = jnp.concatenate([v_prev, vb], axis=2)
    s = jnp.einsum('bnqkgd,bnskd->bnkgqs', qb, kw).astype(jnp.float32) * ATT_SCALE
    qpos = jnp.arange(L)[:, None] + L
    kpos = jnp.arange(2 * L)[None, :]
    diff = qpos - kpos
    band = (diff >= 0) & (diff < WINDOW)
    key_exists = (jnp.arange(nB)[:, None] * L - L + jnp.arange(2 * L)[None, :]) >= 0
    mask = band[None, :, :] & key_exists[:, None, :]
    s = jnp.where(mask[None, :, None, None, :, :], s, NEG_INF)
    sink = jnp.broadcast_to(sinks.astype(jnp.float32).reshape(KV, G)[None, None, :, :, None, None],
                            s.shape[:-1] + (1,))
    p = jax.nn.softmax(jnp.concatenate([s, sink], axis=-1), axis=-1)[..., :-1]
    o = jnp.einsum('bnkgqs,bnskd->bnqkgd', p.astype(v.dtype), vw)
    return o.reshape(B, S, KV * G * D)


def hybrid_layer(x, w_in, w_out, g_pre, g_post, lb, g_head, sinks):
    B, S, _ = x.shape
    h = rms_norm(x, g_pre)
    proj = jnp.einsum('bsd,de->bse', h, w_in)
    q_h, f_h, i_h, z_h, q_a, k_a, v_a, z_a = jnp.split(proj, SPLIT_POINTS, axis=-1)

    o_h = hgrn2(q_h, f_h, i_h, lb)
    o_h = rms_norm(o_h, g_head).reshape(B, S, HG_WIDTH).astype(z_h.dtype) * jax.nn.silu(z_h)

    pos = jnp.arange(S)
    q_a = rotary(q_a.reshape(B, S, ATT_HEADS, ATT_HEAD_DIM), pos)
    k_a = rotary(k_a.reshape(B, S, ATT_KV_HEADS, ATT_HEAD_DIM), pos)
    v_a = v_a.reshape(B, S, ATT_KV_HEADS, ATT_HEAD_DIM)
    o_a = sliding_window_attention(q_a, k_a, v_a, sinks).astype(z_a.dtype) * jax.nn.silu(z_a)

    y = jnp.einsum('bse,ed->bsd', jnp.concatenate([o_h, o_a], axis=-1), w_out)
    return x + rms_norm(y, g_post)


def setup_inputs(seed: int = 0) -> dict:
    key = jax.random.key(seed)
    ks = jax.random.split(key, 9)
    x = jax.random.normal(ks[0], (BATCH, SEQ, D_MODEL), dtype=jnp.float32)
    w_in = jax.random.normal(ks[1], (DEPTH, D_MODEL, IN_WIDTH), dtype=jnp.float32) * D_MODEL ** -0.5
    w_out = jax.random.normal(ks[2], (DEPTH, MIX_WIDTH, D_MODEL), dtype=jnp.float32) * MIX_WIDTH ** -0.5
    g_pre = 1.0 + 0.05 * jax.random.normal(ks[3], (DEPTH, D_MODEL), dtype=jnp.float32)
    g_post = 1.0 + 0.05 * jax.random.normal(ks[4], (DEPTH, D_MODEL), dtype=jnp.float32)
    lb_param = 0.1 * jax.random.normal(ks[5], (DEPTH, HG_WIDTH), dtype=jnp.float32)
    g_head = 1.0 + 0.05 * jax.random.normal(ks[6], (DEPTH, HG_HEAD_DIM), dtype=jnp.float32)
    sinks = jax.random.normal(ks[7], (DEPTH, ATT_HEADS), dtype=jnp.float32)
    return {"x": x, "w_in": w_in, "w_out": w_out, "g_pre": g_pre, "g_post": g_post,
            "lb_param": lb_param, "g_head": g_head, "sinks": sinks}


def reference(x, w_in, w_out, g_pre, g_post, lb_param, g_head, sinks):
    p = jax.nn.softmax(lb_param.astype(jnp.float32), axis=0)
    lower_bounds = jnp.cumsum(p, axis=0) - p[0:1]
    for l in range(DEPTH):
        x = hybrid_layer(x, w_in[l], w_out[l], g_pre[l], g_post[l], lower_bounds[l], g_head[l], sinks[l])
    return x
```

```python
import math
from contextlib import ExitStack

import numpy as np
import ml_dtypes
import concourse.bass as bass
import concourse.mybir as mybir
from concourse.bass_utils import run_bass_kernel_spmd

F32 = mybir.dt.float32
BF16 = mybir.dt.bfloat16
AF = mybir.ActivationFunctionType
ALU = mybir.AluOpType
AX = mybir.AxisListType

D = 1024
IN_W = 6400
NEG = -30000.0
EPS = 1e-6


class _Rec:
    def __getattr__(self, name):
        return lambda *a, **k: (name, a, k)


_REC = _Rec()


class Op:
    __slots__ = ('idx', 'e', 'recs', 'agent', 'inc', 'deps', 'dur', 'lat', 'val')


def _free_elems(rec):
    name, a, k = rec
    ap = k.get('out', None)
    if ap is None and a:
        ap = a[0]
    try:
        shp = ap.shape
        n = 1
        for d in shp[1:]:
            n *= int(d)
        return n, ap
    except Exception:
        return 64, None


class Prog:
    ENG = ('pe', 'act', 'dve', 'pool', 'sp')
    WINDOW = 1400
    MARGIN = 0.35
    USE_BLEV = True

    def __init__(self, nc, es):
        self.nc = nc
        self.es = es
        self.sem = {}
        for e in self.ENG:
            self.sem[e] = es.enter_context(nc.semaphore("s_" + e))
        self.ops = []
        self.lastw = {}
        self.readers = {}
        self.bank = {}
        self.last_sp = None
        self.last_drain = 0

    def dma_agent(self, name):
        a = "dma_" + name
        if a not in self.sem:
            self.sem[a] = self.es.enter_context(self.nc.semaphore("s_" + a))
        return a

    def _cost(self, e, recs, is_dma):
        if not recs:
            return 0.0, 0.0
        if is_dma:
            tot = 0
            for r in recs:
                n, ap = _free_elems(r)
                tot += n * 128 * 4
            return 0.06 * len(recs), 2.5 + tot / 150e3
        name = recs[0][0]
        n, ap = _free_elems(recs[0])
        if e == 'pe':
            if name == 'transpose':
                return 0.08, 0.25
            rhs = recs[0][2].get('rhs')
            ncol = 1
            for d in rhs.shape[1:]:
                ncol *= int(d)
            f32 = (rhs.dtype == F32)
            return max(0.06, 0.01 + ncol / 2400.0) * (4 if f32 else 1), 0.25
        if e == 'act':
            return 0.22 + n / 1150.0, 0.15
        if e == 'dve':
            return 0.12 + n / 900.0, 0.15
        if e == 'pool':
            return 0.25 + n / 520.0, 0.2
        return 0.1, 0.1

    def _add(self, e, recs, agent, inc, reads, writes, banks):
        op = Op()
        op.idx = len(self.ops)
        op.e = e
        op.recs = recs
        op.agent = agent
        op.inc = inc
        deps = set()
        for k in reads:
            if k in self.lastw:
                deps.add(self.lastw[k])
        for k in writes:
            if k in self.lastw:
                deps.add(self.lastw[k])
            deps.update(self.readers.get(k, ()))
        for b in banks:
            for ag, i in self.bank.get(b, {}).items():
                deps.add(i)
        op.deps = deps
        op.dur, op.lat = self._cost(e, recs, agent is not None and agent.startswith('dma_'))
        self.ops.append(op)
        for k in reads:
            self.readers.setdefault(k, []).append(op.idx)
        for k in writes:
            self.lastw[k] = op.idx
            self.readers[k] = []
        for b in banks:
            self.bank.setdefault(b, {})[e] = op.idx
        return op

    def emit(self, e, fn, reads=(), writes=(), banks=(), dma=None):
        if dma is not None:
            self._add(e, [fn(_REC)], dma, 16, reads, writes, banks)
        else:
            self._add(e, [fn(_REC)], e, 1, reads, writes, banks)

    def dma_group(self, e, agent, fns, reads=(), writes=()):
        self._add(e, [fn(_REC) for fn in fns], agent, 16, reads, writes, ())

    def drain(self, e, agents=None):
        op = self._add(e, [], None, 0, (), (), ())
        op.deps.update(range(self.last_drain, op.idx))
        self.last_drain = op.idx

    def _schedule(self):
        ops = self.ops
        n = len(ops)
        indeg = [0] * n
        succ = [[] for _ in range(n)]
        for op in ops:
            indeg[op.idx] = len(op.deps)
            for d in op.deps:
                succ[d].append(op.idx)
        ready = {e: [] for e in self.ENG}
        for op in ops:
            if indeg[op.idx] == 0:
                ready[op.e].append(op.idx)
        blev = [0.0] * n
        for op in reversed(ops):
            m = 0.0
            for sx in succ[op.idx]:
                if blev[sx] > m:
                    m = blev[sx]
            blev[op.idx] = m + op.dur + op.lat
        free = {e: 0.0 for e in self.ENG}
        done = [0.0] * n
        avail = [0.0] * n
        order = {e: [] for e in self.ENG}
        scheduled = [False] * n
        sp_list = [op.idx for op in ops if op.e == 'sp']
        availby = [None] * n
        self.dbg_start = [0.0] * n
        self.dbg_idle = [None] * n
        sp_ptr = [0]
        lo = 0
        nsched = 0
        while nsched < n:
            while lo < n and scheduled[lo]:
                lo += 1
            best = None
            for e in self.ENG:
                r = ready[e]
                if not r:
                    continue
                r.sort()
                if e == 'sp':
                    while sp_ptr[0] < len(sp_list) and scheduled[sp_list[sp_ptr[0]]]:
                        sp_ptr[0] += 1
                    cands = [sp_list[sp_ptr[0]]] if (sp_ptr[0] < len(sp_list) and sp_list[sp_ptr[0]] in r) else []
                else:
                    cands = r[:24]
                for i in cands:
                    if i >= lo + self.WINDOW:
                        break
                    st = max(free[e], avail[i])
                    key = (round(st, 1), -blev[i] if self.USE_BLEV else i)
                    if best is None or key < best[0]:
                        best = (key, e, i)
            if best is None:
                cand = [(r[0], e) for e, r in ready.items() if r and e != 'sp']
                if sp_ptr[0] < len(sp_list) and sp_list[sp_ptr[0]] in ready['sp']:
                    cand.append((sp_list[sp_ptr[0]], 'sp'))
                i, e = min(cand)
                best = ((max(free[e], avail[i]), 0), e, i)
            (st, _), e, i = best
            ready[e].remove(i)
            op = ops[i]
            self.dbg_start[i] = st
            self.dbg_idle[i] = (st - free[e], availby[i]) if avail[i] > free[e] else (0.0, None)
            free[e] = st + op.dur
            done[i] = st + op.dur + op.lat
            scheduled[i] = True
            nsched += 1
            order[e].append(i)
            for sidx in succ[i]:
                indeg[sidx] -= 1
                dn = done[i]
                if e == 'pe' and ops[sidx].e == 'pe':
                    dn = st + op.dur
                elif ops[sidx].e != e:
                    dn = done[i] + self.MARGIN
                if dn > avail[sidx]:
                    avail[sidx] = dn
                    availby[sidx] = i
                if indeg[sidx] == 0:
                    ready[ops[sidx].e].append(sidx)
        self.sim_time = max(done) if n else 0.0
        return order

    def flush(self):
        nc = self.nc
        ops = self.ops
        order = self._schedule()
        cnt = {}
        for e in self.ENG:
            for i in order[e]:
                op = ops[i]
                if op.agent is None:
                    op.val = 0
                    continue
                cnt[op.agent] = cnt.get(op.agent, 0) + op.inc * len(op.recs)
                op.val = cnt[op.agent]
        queues = {}
        for e in self.ENG:
            q = []
            seen = {}
            for i in order[e]:
                op = ops[i]
                need = {}
                for d in op.deps:
                    dop = ops[d]
                    if dop.agent is None:
                        continue
                    if dop.agent == e and e in ('pe', 'sp'):
                        continue
                    if need.get(dop.agent, 0) < dop.val:
                        need[dop.agent] = dop.val
                for ag, v in need.items():
                    if seen.get(ag, 0) < v:
                        q.append(('w', ag, v))
                        seen[ag] = v
                for rec in op.recs:
                    q.append(('i', rec, op.agent, op.inc))
            queues[e] = q
        self.n_inst = {e: len(q) for e, q in queues.items()}
        with nc.Block() as block:
            def mk(e):
                def body(eng):
                    for it in queues[e]:
                        if it[0] == 'w':
                            eng.wait_ge(self.sem[it[1]], it[2])
                        else:
                            name, a, k = it[1]
                            getattr(eng, name)(*a, **k).then_inc(self.sem[it[2]], it[3])
                return body
            block.sync(mk('sp'))
            block.tensor(mk('pe'))
            block.scalar(mk('act'))
            block.vector(mk('dve'))
            block.gpsimd(mk('pool'))


def build_program(n_layers=2, n_seq=2, n_tiles=16, dbg=None):
    NT = n_seq * n_tiles
    ROWS = NT * 128
    nc = bass.Bass("TRN2", target_bir_lowering=False)
    dt_in = lambda name, shape, dt=F32: nc.dram_tensor(name, shape, dt, kind="ExternalInput").ap()
    x_in = dt_in("x", [ROWS, D])
    w_in = dt_in("w_in", [2, D, IN_W])
    w_out = dt_in("w_out", [2, 2 * D, D])
    g_pre = dt_in("g_pre", [2, D])
    g_post = dt_in("g_post", [2, D])
    lb_param = dt_in("lb_param", [2, D])
    g_head = dt_in("g_head", [2, 128])
    sinks = dt_in("sinks", [2, 16])
    c_ident = dt_in("c_ident", [128, 128], BF16)
    c_tri = dt_in("c_tri", [128, 128])
    c_tribf = dt_in("c_tribf", [128, 128], BF16)
    c_csel = dt_in("c_csel", [128, 2])
    c_mprev = dt_in("c_mprev", [128, 512], BF16)
    c_mcur = dt_in("c_mcur", [128, 512], BF16)
    c_cs = dt_in("c_cs", [n_tiles * 128, 64])
    out = nc.dram_tensor("out", [ROWS, D], F32, kind="ExternalOutput").ap()
    x1 = nc.dram_tensor("x1", [ROWS, D], F32, kind="Internal").ap()
    otd = nc.dram_tensor("otd", [NT, 128, 2048], BF16, kind="Internal").ap()
    dbg_out = None
    if dbg is not None:
        dbg_out = {k: nc.dram_tensor("dbg_" + k, [128, 1024], F32, kind="ExternalOutput").ap() for k in dbg[1]}
    dbg_tile = dbg[0] if dbg is not None else -1

    with ExitStack() as es:
        P = Prog(nc, es)
        sb = lambda name, shape, dt: es.enter_context(nc.sbuf_tensor(name, shape, dt))
        ps = lambda name, shape, dt: es.enter_context(nc.psum_tensor(name, shape, dt))
        Wbuf = sb("Wbuf", [128, 8 * IN_W], BF16)
        Win = Wbuf[:].rearrange("p (c n) -> p c n", c=8)
        Wout = Wbuf[:, 0:16 * D].rearrange("p (c n) -> p c n", c=16)
        xt = [sb("xt%d" % i, [128, D], F32) for i in range(2)]
        cs = [sb("cs%d" % i, [128, 64], F32) for i in range(2)]
        xn2 = [sb("xn_%d" % i, [128, D], BF16) for i in range(2)]
        xT2 = [sb("xT_%d" % i, [128, 8, 128], BF16) for i in range(2)]
        tE = sb("tE", [128, 512], F32)
        tL = sb("tL", [128, 512], F32)
        tF = sb("tF", [128, 512], F32)
        tQ = sb("tQ", [128, 512], F32)
        tZ = sb("tZ", [128, 512], F32)
        tR = sb("tR", [128, 1024], F32)
        junk = sb("junk", [128, 512], BF16)
        ssB = sb("ssB", [128, 2], F32)
        rstdB = sb("rstdB", [128, 1], F32)
        kt2 = [sb("kt_%d" % i, [128, D], BF16) for i in range(2)]
        qt2 = [sb("qt_%d" % i, [128, D], BF16) for i in range(2)]
        Vb2 = [sb("Vb_%d" % i, [128, D], BF16) for i in range(2)]
        gate_h2 = [sb("gate_h_%d" % i, [128, D], BF16) for i in range(2)]
        gate_a2 = [sb("gate_a_%d" % i, [128, D], BF16) for i in range(2)]
        trb = sb("trb", [128, 16, 128], BF16)
        qa2 = [sb("qa_%d" % i, [128, D], BF16) for i in range(2)]
        qaT = sb("qaT", [128, 8, 128], BF16)
        kdup2 = [sb("kdup_%d" % i, [128, 2, 2, 128], BF16) for i in range(2)]
        kT2 = [sb("kT2_%d" % i, [128, 4, 128], BF16) for i in range(2)]
        Vaug = [sb("Vaug%d" % i, [128, 2, 65], BF16) for i in range(3)]
        A_bf = sb("A_bf", [128, 4, 128], BF16)
        tS = sb("tS", [128, 512], F32)
        S = sb("S", [128, D], F32)
        S_bf = sb("S_bf", [128, D], BF16)
        PTp = sb("PTp", [128, 4, 128], BF16)
        PTc = sb("PTc", [128, 4, 128], BF16)
        O_bf = sb("O_bf", [128, 2 * D], BF16)
        OT = sb("OT", [128, 16, 128], BF16)
        yt = sb("yt", [128, D], F32)
        LB = sb("LB", [128, D], F32)
        OML = sb("OML", [128, D], BF16)
        gpost_b = sb("gpost_b", [128, D], F32)
        ghead_b = sb("ghead_b", [128, 128], F32)
        ident = sb("ident", [128, 128], BF16)
        tri = sb("tri", [128, 128], F32)
        tribf = sb("tribf", [128, 128], BF16)
        csel = sb("csel", [128, 2], F32)
        mprev = sb("mprev", [128, 512], BF16)
        mcur = sb("mcur", [128, 512], BF16)
        gpre_b = sb("gpre_b", [128, D], F32)
        esink = sb("esink", [128, 16], F32)
        a_sb2 = [sb("a_sb%d" % i, [128, 16], F32) for i in range(2)]
        ss = sb("ss", [128, 2], F32)
        rstd = sb("rstd", [128, 1], F32)
        ssq = sb("ssq", [128, 4], F32)
        rstdh = sb("rstdh", [128, 4], F32)
        den = sb("den", [128, 4], F32)
        rden = sb("rden", [128, 4], F32)
        pj = [ps("pj%d" % i, [128, 512], F32) for i in range(3)]
        pjb = [p_.bitcast(BF16) for p_ in pj]
        pb = ps("pb", [128, 512], F32)
        ptr = ps("ptr", [128, 1024], BF16)
        pms = ps("pms", [128, 512], F32)
        pc = [ps("pc%d" % i, [128, 512], F32) for i in range(2)]
        PJB = ['pj0', 'pj1', 'pj2']
        PCB = ['pc0', 'pc1']

        ag_c = P.dma_agent("const")
        P.dma_group('sp', ag_c, [
            lambda e: e.dma_start(out=ident[:], in_=c_ident),
            lambda e: e.dma_start(out=tri[:], in_=c_tri),
            lambda e: e.dma_start(out=tribf[:], in_=c_tribf),
            lambda e: e.dma_start(out=csel[:], in_=c_csel),
            lambda e: e.dma_start(out=mprev[:], in_=c_mprev),
            lambda e: e.dma_start(out=mcur[:], in_=c_mcur),
        ], writes=['ident', 'tri', 'tribf', 'csel', 'mprev', 'mcur'])
        for i in range(3):
            P.emit('pool', lambda e, i=i: e.memset(Vaug[i][:], 1.0), writes=['Vaug%d' % i])
        for i in range(2):
            P.emit('pool', lambda e, i=i: e.memset(kdup2[i][:], 0.0), writes=['kdup%d' % i])

        ag_w = P.dma_agent("w")
        ag_p = P.dma_agent("par")
        ag_x = [P.dma_agent("x0"), P.dma_agent("x1")]
        ag_ot = P.dma_agent("ot")
        ag_otl3 = [P.dma_agent("otl%d" % i) for i in range(3)]
        ag_x3 = [P.dma_agent("xb%d" % i) for i in range(3)]
        ag_st = [P.dma_agent("st0"), P.dma_agent("st1")]
        ag_dbg = P.dma_agent("dbg")

        def dbg_dump(name, src_ap, key, via=None):
            if dbg_out is None or name not in dbg_out:
                return
            P.emit('sp', lambda e: e.dma_start(out=dbg_out[name], in_=src_ap), reads=[key], writes=['dbg_' + name], dma=ag_dbg)
            P.drain('sp', [ag_dbg])

        pjrot = [0]
        WBLK = [(1024, 512), (2048, 512), (3072, 512), (0, 512), (1536, 512), (2560, 512), (3584, 512), (512, 512),
                (4096, 512), (4608, 512), (5120, 256), (5376, 512), (5888, 512)]
        WKEY = {c0: bi for bi, (c0, n_) in enumerate(WBLK)}
        ag_wlo = [P.dma_agent("wlo%d" % i) for i in range(len(WBLK))]
        ag_whi = [P.dma_agent("whi%d" % i) for i in range(len(WBLK))]
        ALL_WLO = ['Wlo%d' % i for i in range(len(WBLK))]

        def load_w_in(l_, part):
            ca, cb = (0, 3) if part == 'lo' else (3, 8)
            for bi, (c0, n_) in enumerate(WBLK):
                fns_ = [lambda e, ca=ca, cb=cb, c0=c0, n_=n_: e.dma_start(out=Win[:, ca:cb, c0:c0 + n_],
                                                                         in_=w_in[l_, ca * 128:cb * 128, c0:c0 + n_].rearrange("(c p) n -> p c n", p=128))]
                P.dma_group('pool', (ag_wlo if part == 'lo' else ag_whi)[bi], fns_, writes=['W%s%d' % (part, bi)])

        for l in range(n_layers):
            x_src = x_in if l == 0 else x1
            x_dst = x1 if l < n_layers - 1 else out
            P.drain('sp')
            P.dma_group('sp', ag_p, [
                lambda e: e.dma_start(out=gpre_b[:], in_=g_pre[l:l + 1, :].partition_broadcast(128)),
                lambda e: e.dma_start(out=gpost_b[:], in_=g_post[l:l + 1, :].partition_broadcast(128)),
                lambda e: e.dma_start(out=ghead_b[:], in_=g_head[l:l + 1, :].partition_broadcast(128)),
                lambda e: e.dma_start(out=esink[:], in_=sinks[l:l + 1, :].partition_broadcast(128)),
                lambda e: e.dma_start(out=S[:], in_=lb_param[0:1, :].partition_broadcast(128)),
                lambda e: e.dma_start(out=yt[:], in_=lb_param[1:2, :].partition_broadcast(128)),
            ], writes=['gpre_b', 'gpost_b', 'ghead_b', 'esink', 'S', 'yt'])
            P.emit('act', lambda e: e.activation(out=esink[:], in_=esink[:], func=AF.Exp), reads=['esink'], writes=['esink'])
            P.emit('dve', lambda e: e.tensor_tensor(out=LB[:], in0=yt[:], in1=S[:], op=ALU.subtract), reads=['yt', 'S'], writes=['LB'])
            P.emit('act', lambda e: e.activation(out=S[:], in_=LB[:], func=AF.Exp), reads=['LB'], writes=['S'])
            P.emit('act', lambda e: e.activation(out=yt[:], in_=LB[:], func=AF.Exp, scale=-1.0), reads=['LB'], writes=['yt'])
            P.emit('dve', lambda e: e.tensor_scalar_add(out=S[:], in0=S[:], scalar1=1.0), reads=['S'], writes=['S'])
            P.emit('dve', lambda e: e.reciprocal(out=S[:], in_=S[:]), reads=['S'], writes=['S'])
            P.emit('dve', lambda e: e.tensor_scalar_add(out=yt[:], in0=yt[:], scalar1=1.0), reads=['yt'], writes=['yt'])
            P.emit('dve', lambda e: e.reciprocal(out=yt[:], in_=yt[:]), reads=['yt'], writes=['yt'])
            if l == 0:
                P.emit('dve', lambda e: e.tensor_tensor(out=LB[:], in0=S[:], in1=S[:], op=ALU.subtract), reads=['S'], writes=['LB'])
            else:
                P.emit('dve', lambda e: e.tensor_tensor(out=LB[:], in0=S[:], in1=yt[:], op=ALU.add), reads=['S', 'yt'], writes=['LB'])
                P.emit('dve', lambda e: e.tensor_tensor(out=LB[:], in0=LB[:], in1=S[:], op=ALU.subtract), reads=['S', 'LB'], writes=['LB'])
            P.emit('act', lambda e: e.activation(out=OML[:], in_=LB[:], func=AF.Ln, scale=-1.0, bias=1.0), reads=['LB'], writes=['OML'])

            load_w_in(l, 'lo')
            if l == 0:
                load_w_in(l, 'hi')

            def tile_load(ti):
                s, T = divmod(ti, n_tiles)
                par = ti % 2
                r0 = ti * 128
                XT = xt[par]
                CS = cs[par]
                P.dma_group('sp', ag_x[par], [
                    lambda e, XT=XT, r0=r0: e.dma_start(out=XT[:], in_=x_src[r0:r0 + 128, :]),
                    lambda e, CS=CS, T=T: e.dma_start(out=CS[:], in_=c_cs[T * 128:(T + 1) * 128, :]),
                ], writes=['xt%d' % par, 'cs%d' % par])

            def tile_chains(ti):
                if True:
                    s, T = divmod(ti, n_tiles)
                    par = ti % 2
                    r0 = ti * 128
                    XT = xt[par]
                    kx = 'xt%d' % par
                    CS = cs[par]
                    xn = xn2[par]
                    K_xn = 'xn%d' % par
                    xT = xT2[par]
                    K_xT = 'xT%d' % par
                    kt = kt2[par]
                    K_kt = 'kt%d' % par
                    qt = qt2[par]
                    K_qt = 'qt%d' % par
                    Vb = Vb2[par]
                    K_Vb = 'Vb%d' % par
                    gate_h = gate_h2[par]
                    K_gate_h = 'gate_h%d' % par
                    gate_a = gate_a2[par]
                    K_gate_a = 'gate_a%d' % par
                    qa = qa2[par]
                    K_qa = 'qa%d' % par
                    kdup = kdup2[par]
                    K_kdup = 'kdup%d' % par
                    kcs = 'cs%d' % par
                    VA = Vaug[ti % 3]
                    kva = 'Vaug%d' % (ti % 3)
                    a_sb = a_sb2[par]
                    K_a_sb = 'a_sb%d' % par
                    P.emit('act', lambda e, XT=XT: e.activation(out=junk[:], in_=XT[:, 0:512], func=AF.Square, accum_out=ss[:, 0:1]),
                           reads=[kx], writes=['junk', 'ss0'])
                    P.emit('act', lambda e, XT=XT: e.activation(out=junk[:], in_=XT[:, 512:1024], func=AF.Square, accum_out=ss[:, 1:2]),
                           reads=[kx], writes=['junk', 'ss1'])
                    P.emit('dve', lambda e: e.tensor_tensor(out=rstd[:], in0=ss[:, 0:1], in1=ss[:, 1:2], op=ALU.add), reads=['ss0', 'ss1'], writes=['rstd'])
                    P.emit('act', lambda e: e.activation(out=rstd[:], in_=rstd[:], func=AF.Ln, scale=1.0 / D, bias=EPS), reads=['rstd'], writes=['rstd'])
                    P.emit('act', lambda e: e.activation(out=rstd[:], in_=rstd[:], func=AF.Exp, scale=-0.5), reads=['rstd'], writes=['rstd'])
                    P.emit('dve', lambda e, XT=XT: e.scalar_tensor_tensor(out=xn[:], in0=XT[:], scalar=rstd[:, 0:1], in1=gpre_b[:], op0=ALU.mult, op1=ALU.mult),
                           reads=[kx, 'rstd', 'gpre_b'], writes=[K_xn])
                    ix = pjrot[0] % 3
                    pjrot[0] += 1
                    for c in range(8):
                        P.emit('pe', lambda e, c=c, ix=ix: e.transpose(out=pjb[ix][:, c * 128:(c + 1) * 128], in_=xn[:, c * 128:(c + 1) * 128], identity=ident[:]),
                               reads=[K_xn, 'ident'], writes=[PJB[ix]], banks=[PJB[ix]])
                    P.emit('dve', lambda e, ix=ix: e.tensor_copy(out=xT[:].rearrange("p c t -> p (c t)"), in_=pjb[ix][:]),
                           reads=[PJB[ix]], writes=[K_xT + '_%d' % c for c in range(8)], banks=[PJB[ix]])

                    def proj(c0, n):
                        i = pjrot[0] % 3
                        pjrot[0] += 1
                        for c in range(8):
                            P.emit('pe', lambda e, c=c, i=i: e.matmul(pj[i][:, 0:n], lhsT=xT[:, c, :], rhs=Win[:, c, c0:c0 + n], start=(c == 0), stop=(c == 7)),
                                   reads=[K_xT + '_%d' % c, ('Wlo%d' if c < 3 else 'Whi%d') % WKEY[c0]], writes=[PJB[i]], banks=[PJB[i]])
                        return pj[i], PJB[i]

                    for h in range(2):
                        hs = slice(h * 512, (h + 1) * 512)
                        Pf, kf = proj(1024 + h * 512, 512)
                        P.emit('act', lambda e, Pf=Pf: e.activation(out=tE[:], in_=Pf[:], func=AF.Exp, scale=-1.0), reads=[kf], writes=['tE'], banks=[kf])
                        P.emit('act', lambda e: e.activation(out=tL[:], in_=tE[:], func=AF.Ln, bias=1.0), reads=['tE'], writes=['tL'])
                        P.emit('pool', lambda e, hs=hs: e.tensor_tensor(out=tE[:], in0=tE[:], in1=LB[:, hs], op=ALU.mult), reads=['tE', 'LB'], writes=['tE'])
                        P.emit('act', lambda e: e.activation(out=tE[:], in_=tE[:], func=AF.Ln, bias=1.0), reads=['tE'], writes=['tE'])
                        P.emit('pool', lambda e: e.tensor_tensor(out=tF[:], in0=tE[:], in1=tL[:], op=ALU.subtract), reads=['tE', 'tL'], writes=['tF'])
                        P.emit('pool', lambda e, hs=hs: e.tensor_tensor(out=tL[:], in0=tL[:], in1=OML[:, hs], op=ALU.subtract), reads=['tL', 'OML'], writes=['tL'])
                        P.emit('dve', lambda e, Pf=Pf: e.scalar_tensor_tensor(out=tL[:], in0=Pf[:], scalar=-1.0, in1=tL[:], op0=ALU.mult, op1=ALU.subtract),
                               reads=[kf, 'tL'], writes=['tL'], banks=[kf])
                        P.emit('pe', lambda e: e.matmul(pb[:], lhsT=tri[:], rhs=tF[:], start=True, stop=True), reads=['tri', 'tF'], writes=['pb'], banks=['pb'])
                        for hh in range(4):
                            hd = h * 4 + hh
                            P.emit('pe', lambda e, hh=hh, hd=hd: e.matmul(pms[:, 2 * hd:2 * hd + 2], lhsT=tF[:, hh * 128:(hh + 1) * 128], rhs=csel[:], start=True, stop=True),
                                   reads=['tF', 'csel'], writes=['pms_bl'], banks=['pms'])
                        P.emit('dve', lambda e: e.tensor_tensor(out=tL[:], in0=tL[:], in1=pb[:], op=ALU.subtract), reads=['tL', 'pb'], writes=['tL'], banks=['pb'])
                        P.emit('act', lambda e, hs=hs: e.activation(out=kt[:, hs], in_=tL[:], func=AF.Exp), reads=['tL'], writes=[K_kt])
                        Pi, ki = proj(2048 + h * 512, 512)
                        P.emit('act', lambda e, Pi=Pi, hs=hs: e.activation(out=Vb[:, hs], in_=Pi[:], func=AF.Copy), reads=[ki], writes=[K_Vb], banks=[ki])
                        Pz, kz = proj(3072 + h * 512, 512)
                        P.emit('act', lambda e, Pz=Pz: e.activation(out=tZ[:], in_=Pz[:], func=AF.Exp, scale=-1.0), reads=[kz], writes=['tZ'], banks=[kz])
                        P.emit('act', lambda e: e.activation(out=tZ[:], in_=tZ[:], func=AF.Ln, bias=1.0), reads=['tZ'], writes=['tZ'])
                        P.emit('act', lambda e: e.activation(out=tZ[:], in_=tZ[:], func=AF.Exp, scale=-1.0), reads=['tZ'], writes=['tZ'])
                        P.emit('dve', lambda e, Pz=Pz: e.tensor_tensor(out=tZ[:], in0=Pz[:], in1=tZ[:], op=ALU.mult), reads=[kz, 'tZ'], writes=['tZ'], banks=[kz])
                        P.emit('pool', lambda e, hs=hs: e.tensor_tensor(out=gate_h[:, hs].rearrange("p (a v) -> p a v", a=4), in0=tZ[:].rearrange("p (a v) -> p a v", a=4),
                                                                       in1=ghead_b[:].unsqueeze(1).broadcast_to([128, 4, 128]), op=ALU.mult),
                               reads=['tZ', 'ghead_b'], writes=[K_gate_h])
                        Pq, kq = proj(h * 512, 512)
                        P.emit('act', lambda e, Pq=Pq: e.activation(out=tQ[:], in_=Pq[:], func=AF.Exp, scale=-1.0), reads=[kq], writes=['tQ'], banks=[kq])
                        P.emit('act', lambda e: e.activation(out=tQ[:], in_=tQ[:], func=AF.Ln, bias=1.0), reads=['tQ'], writes=['tQ'])
                        P.emit('dve', lambda e: e.tensor_tensor(out=tQ[:], in0=pb[:], in1=tQ[:], op=ALU.subtract), reads=['tQ', 'pb'], writes=['tQ'], banks=['pb'])
                        P.emit('act', lambda e: e.activation(out=tQ[:], in_=tQ[:], func=AF.Exp), reads=['tQ'], writes=['tQ'])
                        P.emit('dve', lambda e, Pq=Pq, hs=hs: e.tensor_tensor(out=qt[:, hs], in0=Pq[:], in1=tQ[:], op=ALU.mult), reads=[kq, 'tQ'], writes=[K_qt], banks=[kq])
                    P.emit('act', lambda e: e.activation(out=a_sb[:], in_=pms[:, 0:16], func=AF.Exp), reads=['pms_bl'], writes=[K_a_sb], banks=['pms'])

                    def rope(Psrc, ksrc, nh, dst_fn, kdst):
                        v = Psrc.rearrange("p (a t c) -> p a t c", a=nh, t=2)
                        x1v, x2v = v[:, :, 0, :], v[:, :, 1, :]
                        cosb = CS[:, 0:32].unsqueeze(1).broadcast_to([128, nh, 32])
                        sinb = CS[:, 32:64].unsqueeze(1).broadcast_to([128, nh, 32])
                        n = nh * 32
                        tA = tR[:, 0:n].rearrange("p (a c) -> p a c", a=nh)
                        tB = tR[:, 256:256 + n].rearrange("p (a c) -> p a c", a=nh)
                        tC = tR[:, 512:512 + n].rearrange("p (a c) -> p a c", a=nh)
                        tD = tR[:, 768:768 + n].rearrange("p (a c) -> p a c", a=nh)
                        P.emit('dve', lambda e: e.tensor_tensor(out=tA, in0=x1v, in1=cosb, op=ALU.mult), reads=[ksrc, kcs], writes=['tR0'], banks=[ksrc])
                        P.emit('dve', lambda e: e.tensor_tensor(out=tB, in0=x2v, in1=sinb, op=ALU.mult), reads=[ksrc, kcs], writes=['tR0'], banks=[ksrc])
                        P.emit('dve', lambda e: e.tensor_tensor(out=tC, in0=x2v, in1=cosb, op=ALU.mult), reads=[ksrc, kcs], writes=['tR1'], banks=[ksrc])
                        P.emit('dve', lambda e: e.tensor_tensor(out=tD, in0=x1v, in1=sinb, op=ALU.mult), reads=[ksrc, kcs], writes=['tR1'], banks=[ksrc])
                        P.emit('pool', lambda e: e.tensor_tensor(out=dst_fn(0), in0=tA, in1=tB, op=ALU.subtract), reads=['tR0'], writes=[kdst])
                        P.emit('pool', lambda e: e.tensor_tensor(out=dst_fn(1), in0=tC, in1=tD, op=ALU.add), reads=['tR1'], writes=[kdst])

                    qav = qa[:].rearrange("p (a t c) -> p a t c", a=16, t=2)
                    for h in range(2):
                        Pqa, kqa = proj(4096 + h * 512, 512)
                        rope(Pqa[:], kqa, 8, lambda half, h=h: qav[:, h * 8:(h + 1) * 8, half, :], K_qa)
                    Pkv, kkv = proj(5120, 256)
                    rope(Pkv[:, 0:128], kkv, 2, lambda half: kdup[:, :, 0, half * 32:(half + 1) * 32], K_kdup)
                    P.emit('pool', lambda e: e.tensor_copy(out=kdup[:, :, 1, 64:128], in_=kdup[:, :, 0, 0:64]), reads=[K_kdup], writes=[K_kdup])
                    P.emit('act', lambda e, VA=VA, Pkv=Pkv: e.activation(out=VA[:, :, 0:64], in_=Pkv[:, 128:256].rearrange("p (g c) -> p g c", g=2), func=AF.Copy),
                           reads=[kkv], writes=[kva], banks=[kkv])
                    for h in range(2):
                        hs = slice(h * 512, (h + 1) * 512)
                        Pz, kz = proj(5376 + h * 512, 512)
                        P.emit('act', lambda e, Pz=Pz: e.activation(out=tZ[:], in_=Pz[:], func=AF.Exp, scale=-1.0), reads=[kz], writes=['tZ'], banks=[kz])
                        P.emit('act', lambda e: e.activation(out=tZ[:], in_=tZ[:], func=AF.Ln, bias=1.0), reads=['tZ'], writes=['tZ'])
                        P.emit('act', lambda e: e.activation(out=tZ[:], in_=tZ[:], func=AF.Exp, scale=-1.0), reads=['tZ'], writes=['tZ'])
                        P.emit('dve', lambda e, Pz=Pz, hs=hs: e.tensor_tensor(out=gate_a[:, hs], in0=Pz[:], in1=tZ[:], op=ALU.mult), reads=[kz, 'tZ'], writes=[K_gate_a], banks=[kz])


            def tile_core(ti):
                if True:
                    s, T = divmod(ti, n_tiles)
                    par = ti % 2
                    r0 = ti * 128
                    XT = xt[par]
                    kx = 'xt%d' % par
                    CS = cs[par]
                    xn = xn2[par]
                    K_xn = 'xn%d' % par
                    xT = xT2[par]
                    K_xT = 'xT%d' % par
                    kt = kt2[par]
                    K_kt = 'kt%d' % par
                    qt = qt2[par]
                    K_qt = 'qt%d' % par
                    Vb = Vb2[par]
                    K_Vb = 'Vb%d' % par
                    gate_h = gate_h2[par]
                    K_gate_h = 'gate_h%d' % par
                    gate_a = gate_a2[par]
                    K_gate_a = 'gate_a%d' % par
                    qa = qa2[par]
                    K_qa = 'qa%d' % par
                    kdup = kdup2[par]
                    K_kdup = 'kdup%d' % par
                    kcs = 'cs%d' % par
                    VA = Vaug[ti % 3]
                    kva = 'Vaug%d' % (ti % 3)
                    a_sb = a_sb2[par]
                    K_a_sb = 'a_sb%d' % par
                    if T == 0:
                        P.emit('pool', lambda e: e.memset(S[:], 0.0), writes=['S'])
                        P.emit('pool', lambda e: e.memset(S_bf[:], 0.0), writes=['S_bf'])
                    for (src, ksrc, dst, kdst, nch) in ((qt, K_qt, trb[:, 0:8, :], 'trb_q', 8), (kt, K_kt, trb[:, 8:16, :], 'trb_k', 8), (qa, K_qa, qaT[:], 'qaT', 8)):
                        for c in range(nch):
                            P.emit('pe', lambda e, c=c, src=src: e.transpose(out=ptr[:, c * 128:(c + 1) * 128], in_=src[:, c * 128:(c + 1) * 128], identity=ident[:]),
                                   reads=[ksrc, 'ident'], writes=['ptr'], banks=['ptr'])
                        P.emit('act', lambda e, dst=dst: e.activation(out=dst, in_=ptr[:].rearrange("p (c t) -> p c t", c=8), func=AF.Copy),
                               reads=['ptr'], writes=[kdst], banks=['ptr'])
                    KT = kT2[par]
                    kkt = 'kT2_%d' % par
                    kdf = kdup[:].rearrange("p g u c -> p (g u c)")
                    for c in range(4):
                        P.emit('pe', lambda e, c=c: e.transpose(out=ptr[:, c * 128:(c + 1) * 128], in_=kdf[:, c * 128:(c + 1) * 128], identity=ident[:]),
                               reads=[K_kdup, 'ident'], writes=['ptr'], banks=['ptr'])
                    P.emit('act', lambda e, KT=KT: e.activation(out=KT[:], in_=ptr[:, 0:512].rearrange("p (c t) -> p c t", c=4), func=AF.Copy),
                           reads=['ptr'], writes=[kkt], banks=['ptr'])

                    for hg in range(2):
                        gs = slice(hg * 512, (hg + 1) * 512)
                        for hh in range(4):
                            hd = hg * 4 + hh
                            P.emit('pe', lambda e, hh=hh, hd=hd: e.matmul(pc[0][:, hh * 128:(hh + 1) * 128], lhsT=trb[:, 8 + hd, :], rhs=trb[:, hd, :], start=True, stop=True),
                                   reads=['trb_q', 'trb_k'], writes=['pc0'], banks=['pc0'])
                        P.emit('dve', lambda e: e.tensor_tensor(out=A_bf[:], in0=pc[0][:].rearrange("p (a t) -> p a t", a=4),
                                                                in1=tribf[:].unsqueeze(1).broadcast_to([128, 4, 128]), op=ALU.mult),
                               reads=['pc0', 'tribf'], writes=['A_bf'], banks=['pc0'])
                        for hh in range(4):
                            hd = hg * 4 + hh
                            P.emit('pe', lambda e, hh=hh, hd=hd: e.matmul(pc[1][:, hh * 128:(hh + 1) * 128], lhsT=A_bf[:, hh, :], rhs=Vb[:, hd * 128:(hd + 1) * 128],
                                                                         start=(hh == 0), stop=False, skip_group_check=True),
                                   reads=['A_bf', K_Vb], writes=['pc1'], banks=['pc1'])
                        for hh in range(4):
                            hd = hg * 4 + hh
                            P.emit('pe', lambda e, hh=hh, hd=hd: e.matmul(pc[1][0:64, hh * 128:(hh + 1) * 128], lhsT=trb[:, hd, 0:64], rhs=S_bf[:, hd * 128:(hd + 1) * 128],
                                                                         start=False, stop=False, skip_group_check=True),
                                   reads=['trb_q', 'S_bf'], writes=['pc1'], banks=['pc1'])
                        for ch in range(2):
                            rows = slice(ch * 64, (ch + 1) * 64)
                            for hh in range(4):
                                hd = hg * 4 + hh
                                P.emit('pe', lambda e, hh=hh, hd=hd, rows=rows: e.matmul(pc[0][:, hh * 128:(hh + 1) * 128], lhsT=kt[rows, hd * 128:(hd + 1) * 128],
                                                                                        rhs=Vb[rows, hd * 128:(hd + 1) * 128], start=True, stop=True),
                                       reads=[K_kt, K_Vb, 'A_bf'], writes=['pc0'], banks=['pc0'])
                            P.emit('dve', lambda e, gs=gs: e.tensor_tensor(out=tS[:], in0=pc[0][:], in1=S[:, gs], op=ALU.add), reads=['pc0', 'S'], writes=['tS'], banks=['pc0'])
                            av = a_sb[:].rearrange("p (a c) -> p a c", c=2)[:, hg * 4:(hg + 1) * 4, ch:ch + 1].broadcast_to([128, 4, 128])
                            P.emit('dve', lambda e, gs=gs, av=av: e.tensor_tensor(out=S_bf[:, gs].rearrange("p (a v) -> p a v", a=4), in0=tS[:].rearrange("p (a v) -> p a v", a=4),
                                                                                 in1=av, op=ALU.mult), reads=['tS', K_a_sb], writes=['S_bf'])
                            P.emit('pool', lambda e, gs=gs, av=av: e.tensor_tensor(out=S[:, gs].rearrange("p (a v) -> p a v", a=4), in0=tS[:].rearrange("p (a v) -> p a v", a=4),
                                                                                  in1=av, op=ALU.mult), reads=['tS', K_a_sb], writes=['S'])
                            if ch == 0:
                                for hh in range(4):
                                    hd = hg * 4 + hh
                                    P.emit('pe', lambda e, hh=hh, hd=hd: e.matmul(pc[1][64:128, hh * 128:(hh + 1) * 128], lhsT=trb[:, hd, 64:128], rhs=S_bf[:, hd * 128:(hd + 1) * 128],
                                                                                 start=False, stop=(hh == 3), skip_group_check=True),
                                           reads=['trb_q', 'S_bf'], writes=['pc1'], banks=['pc1'])
                                for hh in range(4):
                                    P.emit('act', lambda e, hh=hh: e.activation(out=junk[:, 0:128], in_=pc[1][:, hh * 128:(hh + 1) * 128], func=AF.Square, accum_out=ssq[:, hh:hh + 1]),
                                           reads=['pc1'], writes=['junk', 'ssq%d' % hh], banks=['pc1'])
                                P.emit('act', lambda e: e.activation(out=rstdh[:], in_=ssq[:], func=AF.Ln, scale=1.0 / 128, bias=EPS), reads=['ssq0', 'ssq1', 'ssq2', 'ssq3'], writes=['rstdh'])
                                P.emit('act', lambda e: e.activation(out=rstdh[:], in_=rstdh[:], func=AF.Exp, scale=-0.5), reads=['rstdh'], writes=['rstdh'])
                                for hh in range(4):
                                    cs_ = slice(hg * 512 + hh * 128, hg * 512 + (hh + 1) * 128)
                                    P.emit('dve', lambda e, hh=hh, cs_=cs_: e.scalar_tensor_tensor(out=O_bf[:, cs_], in0=pc[1][:, hh * 128:(hh + 1) * 128], scalar=rstdh[:, hh:hh + 1],
                                                                                               in1=gate_h[:, cs_], op0=ALU.mult, op1=ALU.mult),
                                           reads=['pc1', 'rstdh', K_gate_h], writes=['O_bf_h%d' % hg], banks=['pc1'])

                    KTp = kT2[1 - par]
                    kktp = 'kT2_%d' % (1 - par)
                    VAp = Vaug[(ti - 1) % 3]
                    kvap = 'Vaug%d' % ((ti - 1) % 3)
                    Oav = O_bf[:, 1024:2048].rearrange("p (j t c) -> p j t c", j=8, t=2)
                    Gav = gate_a[:].rearrange("p (j t c) -> p j t c", j=8, t=2)
                    esv = esink[:].rearrange("p (j t) -> p j t", t=2)
                    for g in range(2):
                        for pr in range(2):
                            prow = slice(pr * 64, (pr + 1) * 64)
                            rq = qaT[:, 4 * g:4 * g + 4, :]
                            blocks = ([(KTp, kktp, mprev, 'mprev', 0, PTp, 'PTp', VAp, kvap)] if T > 0 else []) + [(KT, kkt, mcur, 'mcur', 1, PTc, 'PTc', VA, kva)]
                            for (K_, kk_, M_, km_, bi, PT_, kpt_, V_, kv_) in blocks:
                                P.emit('pe', lambda e, K_=K_, bi=bi: e.matmul(pb[:], lhsT=K_[:, 2 * g + pr, :], rhs=rq, start=True, stop=False),
                                       reads=[kk_, 'qaT'], writes=['pb'], banks=['pb'])
                                P.emit('pe', lambda e, M_=M_, bi=bi: e.matmul(pb[:], lhsT=ident[:], rhs=M_[:], start=False, stop=True),
                                       reads=[km_, 'ident'], writes=['pb'], banks=['pb'])
                                P.emit('act', lambda e, PT_=PT_, bi=bi: e.activation(out=PT_[:].rearrange("p a t -> p (a t)"), in_=pb[:], func=AF.Exp, scale=0.125),
                                       reads=['pb'], writes=[kpt_], banks=['pb'])
                            first = True
                            for jj in range(4):
                                for (K_, kk_, M_, km_, bi, PT_, kpt_, V_, kv_) in blocks:
                                    P.emit('pe', lambda e, jj=jj, PT_=PT_, V_=V_, first=first: e.matmul(pms[:, 16 + jj * 65:16 + (jj + 1) * 65], lhsT=PT_[:, jj, :], rhs=V_[:, g, :],
                                                                                                    start=first, stop=False, skip_group_check=True),
                                           reads=[kpt_, kv_, K_a_sb], writes=['pms_pv'], banks=['pms'])
                                    first = False
                            pvv = pms[:, 16:16 + 260].rearrange("p (a c) -> p a c", a=4)
                            P.emit('dve', lambda e, pvv=pvv, g=g, pr=pr: e.tensor_tensor(out=den[:], in0=pvv[:, :, 64], in1=esv[:, 4 * g:4 * g + 4, pr], op=ALU.add),
                                   reads=['pms_pv', 'esink'], writes=['den'], banks=['pms'])
                            P.emit('dve', lambda e: e.reciprocal(out=rden[:], in_=den[:]), reads=['den'], writes=['rden'])
                            for jj in range(4):
                                P.emit('dve', lambda e, pvv=pvv, g=g, pr=pr, jj=jj: e.scalar_tensor_tensor(out=Oav[:, 4 * g + jj, pr, :], in0=pvv[:, jj, 0:64], scalar=rden[:, jj:jj + 1],
                                                                                                    in1=Gav[:, 4 * g + jj, pr, :], op0=ALU.mult, op1=ALU.mult),
                                       reads=['pms_pv', 'rden', K_gate_a], writes=['O_bf_a'], banks=['pms'])

                    for r in range(2):
                        for c in range(8):
                            cc = r * 8 + c
                            P.emit('pe', lambda e, c=c, cc=cc: e.transpose(out=ptr[:, c * 128:(c + 1) * 128], in_=O_bf[:, cc * 128:(cc + 1) * 128], identity=ident[:]),
                                   reads=['O_bf_h0', 'O_bf_h1', 'O_bf_a', 'ident'], writes=['ptr'], banks=['ptr'])
                        P.emit('act', lambda e, r=r: e.activation(out=OT[:, r * 8:(r + 1) * 8, :], in_=ptr[:].rearrange("p (c t) -> p c t", c=8), func=AF.Copy),
                               reads=['ptr'], writes=['OT'], banks=['ptr'])
                    P.emit('sp', lambda e, ti=ti: e.dma_start(out=otd[ti], in_=OT[:].rearrange("p c t -> p (c t)")), reads=['OT'], writes=['otd'], dma=ag_ot)
                    if dbg_out is not None and ti == dbg_tile:
                        for nm, ap_, k_ in (("qt", qt[:], K_qt), ("kt", kt[:], K_kt), ("Vb", Vb[:], K_Vb), ("qa", qa[:], K_qa), ("S", S[:], 'S'),
                                            ("Oh", O_bf[:, 0:1024], 'O_bf_h1'), ("Oa", O_bf[:, 1024:2048], 'O_bf_a'), ("gate_h", gate_h[:], K_gate_h), ("gate_a", gate_a[:], K_gate_a)):
                            if nm in dbg_out:
                                P.emit('dve', lambda e, ap_=ap_: e.tensor_copy(out=yt[:], in_=ap_), reads=[k_], writes=['yt'])
                                P.emit('sp', lambda e, nm=nm: e.dma_start(out=dbg_out[nm], in_=yt[:]), reads=['yt'], writes=['dbg_' + nm], dma=ag_dbg)


            tile_load(0)
            if NT > 1:
                tile_load(1)
            tile_chains(0)
            for ti_ in range(NT):
                if ti_ + 2 < NT:
                    tile_load(ti_ + 2)
                if ti_ + 1 < NT:
                    tile_chains(ti_ + 1)
                tile_core(ti_)

            P.drain('sp')
            fns = []
            for c in range(16):
                fns.append(lambda e, c=c: e.dma_start(out=Wout[:, c, :], in_=w_out[l, c * 128:(c + 1) * 128, :]))
            P.dma_group('pool', ag_w, fns, writes=ALL_WLO)
            if l + 1 < n_layers:
                load_w_in(l + 1, 'hi')
            def b_load(ti):
                b3 = ti % 3
                r0 = ti * 128
                XT, kxs = ((xt[0][:], ['xt0']), (xt[1][:], ['xt1']), (tR[:], ['tR0', 'tR1']))[b3]
                OL, kols = ((O_bf[:].rearrange("p (c t) -> p c t", c=16), ['O_bf_a', 'O_bf_h0', 'O_bf_h1']), (OT[:], ['OT']), (trb[:], ['trb_q', 'trb_k']))[b3]
                P.dma_group('sp', ag_x3[b3], [lambda e, XT=XT, r0=r0: e.dma_start(out=XT, in_=x_src[r0:r0 + 128, :])], writes=kxs)
                P.dma_group('sp', ag_otl3[b3], [lambda e, OL=OL, ti=ti: e.dma_start(out=OL.rearrange("p c t -> p (c t)"), in_=otd[ti])], writes=kols)

            def b_compute(ti):
                par = ti % 2
                b3 = ti % 3
                r0 = ti * 128
                XT, kxs = ((xt[0][:], ['xt0']), (xt[1][:], ['xt1']), (tR[:], ['tR0', 'tR1']))[b3]
                OL, kols = ((O_bf[:].rearrange("p (c t) -> p c t", c=16), ['O_bf_a', 'O_bf_h0', 'O_bf_h1']), (OT[:], ['OT']), (trb[:], ['trb_q', 'trb_k']))[b3]
                YT, kyt = ((yt, 'yt'), (S, 'S'))[par]
                SS, kss = ((ss, 'ssx'), (ssB, 'ssB'))[par]
                RS, krs = ((rstd, 'rstd'), (rstdB, 'rstdB'))[par]
                banks2 = ((pj[0], 'pj0'), (pj[1], 'pj1')) if par == 0 else ((pc[0], 'pc0'), (pc[1], 'pc1'))
                for h in range(2):
                    PB, kpb = banks2[h]
                    for c in range(16):
                        P.emit('pe', lambda e, c=c, h=h, OL=OL, PB=PB: e.matmul(PB[:], lhsT=OL[:, c, :], rhs=Wout[:, c, h * 512:(h + 1) * 512], start=(c == 0), stop=(c == 15)),
                               reads=kols + ALL_WLO, writes=[kpb], banks=[kpb])
                    P.emit('act', lambda e, h=h, PB=PB, SS=SS: e.activation(out=junk[:], in_=PB[:], func=AF.Square, accum_out=SS[:, h:h + 1]),
                           reads=[kpb], writes=['junk', kss + str(h)], banks=[kpb])
                P.emit('dve', lambda e, RS=RS, SS=SS: e.tensor_tensor(out=RS[:], in0=SS[:, 0:1], in1=SS[:, 1:2], op=ALU.add), reads=[kss + '0', kss + '1'], writes=[krs])
                P.emit('act', lambda e, RS=RS: e.activation(out=RS[:], in_=RS[:], func=AF.Ln, scale=1.0 / D, bias=EPS), reads=[krs], writes=[krs])
                P.emit('act', lambda e, RS=RS: e.activation(out=RS[:], in_=RS[:], func=AF.Exp, scale=-0.5), reads=[krs], writes=[krs])
                for h in range(2):
                    PB, kpb = banks2[h]
                    P.emit('dve', lambda e, h=h, PB=PB, YT=YT, RS=RS: e.scalar_tensor_tensor(out=YT[:, h * 512:(h + 1) * 512], in0=PB[:], scalar=RS[:, 0:1], in1=gpost_b[:, h * 512:(h + 1) * 512], op0=ALU.mult, op1=ALU.mult),
                           reads=[kpb, krs, 'gpost_b'], writes=[kyt], banks=[kpb])
                P.emit('pool', lambda e, XT=XT, YT=YT: e.tensor_tensor(out=YT[:], in0=YT[:], in1=XT, op=ALU.add), reads=[kyt] + kxs, writes=[kyt])

            def b_store(ti):
                par = ti % 2
                r0 = ti * 128
                YT, kyt = ((yt, 'yt'), (S, 'S'))[par]
                P.emit('sp', lambda e, YT=YT, r0=r0: e.dma_start(out=x_dst[r0:r0 + 128, :], in_=YT[:]), reads=[kyt], writes=['xdst'], dma=ag_st[par])

            for t_ in range(min(3, NT)):
                b_load(t_)
            for ti in range(NT):
                b_compute(ti)
                if ti + 3 < NT:
                    b_load(ti + 3)
                b_store(ti)
        P.drain('sp')
        P.flush()
    return nc


def host_consts(n_tiles=16):
    bf = ml_dtypes.bfloat16
    s = np.arange(128)[:, None]
    t = np.arange(128)[None, :]
    tri = ((s <= t) & ((s // 64) == (t // 64))).astype(np.float32)
    csel = np.zeros((128, 2), np.float32)
    csel[:64, 0] = 1.0
    csel[64:, 1] = 1.0
    mcur1 = np.where(s <= t, 0.0, NEG).astype(np.float32)
    mprev1 = np.where(s > t, 0.0, NEG).astype(np.float32)
    half = 32
    inv_freq = (10000.0 ** (-np.arange(half, dtype=np.float32) / half)).astype(np.float32)
    pos = np.arange(n_tiles * 128, dtype=np.float32)
    ang = (pos[:, None] * inv_freq[None, :]).astype(np.float32)
    cs = np.concatenate([np.cos(ang), np.sin(ang)], axis=1).astype(np.float32)
    return {
        "c_ident": np.eye(128, dtype=np.float32).astype(bf),
        "c_tri": tri,
        "c_tribf": tri.astype(bf),
        "c_csel": csel,
        "c_mprev": np.tile(mprev1, (1, 4)).astype(bf),
        "c_mcur": np.tile(mcur1, (1, 4)).astype(bf),
        "c_cs": np.ascontiguousarray(cs),
    }


def run(x, w_in, w_out, g_pre, g_post, lb_param, g_head, sinks, n_layers=2, n_seq=2, n_tiles=16, dbg=None):
    nc = build_program(n_layers, n_seq, n_tiles, dbg)
    consts = host_consts(n_tiles)
    common = {
        "w_in": np.ascontiguousarray(w_in, dtype=np.float32), "w_out": np.ascontiguousarray(w_out, dtype=np.float32),
        "g_pre": np.ascontiguousarray(g_pre, dtype=np.float32), "g_post": np.ascontiguousarray(g_post, dtype=np.float32),
        "lb_param": np.ascontiguousarray(lb_param, dtype=np.float32), "g_head": np.ascontiguousarray(g_head, dtype=np.float32),
        "sinks": np.ascontiguousarray(sinks, dtype=np.float32),
    }
    common.update(consts)
    rows = n_seq * n_tiles * 128
    in_maps = []
    for c in range(8):
        m = dict(common)
        m["x"] = np.ascontiguousarray(x[c * n_seq:(c + 1) * n_seq].reshape(rows, D), dtype=np.float32)
        in_maps.append(m)
    res = run_bass_kernel_spmd(nc, in_maps, core_ids=list(range(8)))
    outs = [r["out"].reshape(n_seq, n_tiles * 128, D) for r in res.results]
    full = np.concatenate(outs, axis=0)
    if dbg is not None:
        return full, res.results
    return full


def kernel(x, w_in, w_out, g_pre, g_post, lb_param, g_head, sinks):
    x = np.asarray(x)
    out = run(x, np.asarray(w_in), np.asarray(w_out), np.asarray(g_pre), np.asarray(g_post),
              np.asarray(lb_param), np.asarray(g_head), np.asarray(sinks))
    return out.astype(np.float32)
```
